# Optimizing a Trainium2 kernel written in Bass

```python
import math
import jax, jax.numpy as jnp
from jax import lax
import numpy as np

D_MODEL = 1024
BATCH = 32
SEQ = 2048
DEPTH = 1
DEC_BATCH = 128
DEC_SEQ = 4
PAST_LEN = 8192
PAGE_SIZE = 128

HEAD_DIM = 128
SWA_GROUPS = ((128, 1), (512, 4), (2048, 16))
N_SWA = 3
SWA_HEADS = 4
SWA_BAND = 128
GLA_HEADS = 4
GLA_DK = 64
GLA_DV = 128
GLA_RANK = 16
GLA_TAU = 16.0
GLA_CHUNK = 64
MEM_LEN = 256
MEM_HEADS = 4
BRANCH_W = 512
N_BRANCH = 3
D_FF = 2816
ROPE_THETA = 10000.0
EPS = 1e-6
NEG = -1e30

SPLIT_SIZES = (3 * N_SWA * SWA_HEADS * HEAD_DIM,
               GLA_HEADS * GLA_DK,
               GLA_HEADS * GLA_DK,
               GLA_HEADS * GLA_DV,
               GLA_HEADS * GLA_DV,
               GLA_RANK,
               MEM_HEADS * HEAD_DIM)
SPLIT_POINTS = tuple(sum(SPLIT_SIZES[:i + 1]) for i in range(len(SPLIT_SIZES) - 1))
D_IN = sum(SPLIT_SIZES)

kernel_name = 'hybrid_dilated_gla_memory_decode_step'


def _rmsnorm(x, g):
    xf = x.astype(jnp.float32)
    y = xf * lax.rsqrt(jnp.mean(xf * xf, axis=-1, keepdims=True) + EPS)
    return (y * g.astype(jnp.float32)).astype(x.dtype)


def _swiglu(x, w_in, w_out):
    a, b = jnp.split(x @ w_in, 2, axis=-1)
    return (jax.nn.silu(a) * b) @ w_out


def _rope(x, pos):
    half = HEAD_DIM // 2
    inv = ROPE_THETA ** (-jnp.arange(half, dtype=jnp.float32) / half)
    ang = pos.astype(jnp.float32)[:, None] * inv[None, :]
    shape = (1, ang.shape[0]) + (1,) * (x.ndim - 3) + (half,)
    cos = jnp.cos(ang).reshape(shape)
    sin = jnp.sin(ang).reshape(shape)
    xf = x.astype(jnp.float32)
    x1, x2 = xf[..., :half], xf[..., half:]
    return jnp.concatenate([x1 * cos - x2 * sin, x2 * cos + x1 * sin], axis=-1).astype(x.dtype)


def _dilated_prompt(q, k, v, dil):
    B, S, H, E = q.shape
    M = S // dil
    nb = -(-M // SWA_BAND)
    Mp = nb * SWA_BAND

    def strided(t, front):
        t = t.astype(jnp.float32).reshape(B, M, dil, H, E)
        return jnp.pad(t, ((0, 0), (front, Mp - M), (0, 0), (0, 0), (0, 0)))

    def band(t):
        t = strided(t, SWA_BAND).reshape(B, nb + 1, SWA_BAND, dil, H, E)
        return jnp.concatenate([t[:, :-1], t[:, 1:]], axis=2)

    qb = strided(q, 0).reshape(B, nb, SWA_BAND, dil, H, E)
    kb, vb = band(k), band(v)
    s = jnp.einsum('bnqrhe,bnkrhe->bnrhqk', qb, kb) * (E ** -0.5)
    qi = jnp.arange(SWA_BAND)[:, None]
    kj = jnp.arange(2 * SWA_BAND)[None, :]
    dist = SWA_BAND + qi - kj
    key_m = jnp.arange(nb)[:, None, None] * SWA_BAND + kj[None] - SWA_BAND
    valid = (dist >= 0)[None] & (dist <= SWA_BAND)[None] & (key_m >= 0)
    s = jnp.where(valid[None, :, None, None], s, NEG)
    lse = jax.nn.logsumexp(s, axis=-1)
    p = jnp.exp(s - lse[..., None])
    o = jnp.einsum('bnrhqk,bnkrhe->bnqrhe', p, vb)
    o = o.reshape(B, Mp, dil, H, E)[:, :M].reshape(B, S, H, E)
    lse = lse.transpose(0, 1, 4, 2, 3).reshape(B, Mp, dil, H)[:, :M].reshape(B, S, H)
    return o, lse


def _dilated_sample(q, k, v, buf, dil):
    L = buf.shape[1]
    T = q.shape[1]
    k_all = jnp.concatenate([buf[:, :, 0].astype(jnp.float32), k.astype(jnp.float32)], axis=1)
    v_all = jnp.concatenate([buf[:, :, 1].astype(jnp.float32), v.astype(jnp.float32)], axis=1)
    idx = L + jnp.arange(T)[:, None] - dil * jnp.arange(SWA_BAND + 1)[None, :]
    valid = idx >= 0
    idx = jnp.maximum(idx, 0)
    kg = k_all[:, idx]
    vg = v_all[:, idx]
    s = jnp.einsum('bthe,btnhe->bthn', q.astype(jnp.float32), kg) * (q.shape[-1] ** -0.5)
    s = jnp.where(valid[None, :, None, :], s, NEG)
    lse = jax.nn.logsumexp(s, axis=-1)
    p = jnp.exp(s - lse[..., None])
    o = jnp.einsum('bthn,btnhe->bthe', p, vg)
    return o, lse


def _gla(q, k, v, log_a, s0):
    B, T, H, DK = q.shape
    C = GLA_CHUNK if T % GLA_CHUNK == 0 else T
    nc = T // C

    def chunks(t):
        return t.astype(jnp.float32).reshape((B, nc, C) + t.shape[2:]).swapaxes(0, 1)

    causal = jnp.tril(jnp.ones((C, C), dtype=bool))

    def step(s, inp):
        qc, kc, vc, lc = inp
        b = jnp.cumsum(lc, axis=1)
        b_end = b[:, -1]
        qe = qc * jnp.exp(b)
        ke = kc * jnp.exp(-b)
        att = jnp.where(causal, jnp.einsum('bihk,bjhk->bhij', qe, ke), 0.0)
        o = jnp.einsum('bchk,bhkv->bchv', qe, s) + jnp.einsum('bhij,bjhv->bihv', att, vc)
        kd = kc * jnp.exp(b_end[:, None] - b)
        s = s * jnp.exp(b_end)[..., None] + jnp.einsum('bchk,bchv->bhkv', kd, vc)
        return s, o

    s, o = lax.scan(step, s0.astype(jnp.float32), (chunks(q), chunks(k), chunks(v), chunks(log_a)))
    return o.swapaxes(0, 1).reshape(B, T, H, v.shape[-1]), s


def _mem_kv(mem, g_mem, w_mem_kv):
    B = mem.shape[0]
    return (_rmsnorm(mem, g_mem) @ w_mem_kv).reshape(B, MEM_LEN, 2, MEM_HEADS, HEAD_DIM)


def _mem_attend(q, kv):
    s = jnp.einsum('bthe,bmhe->bhtm', q.astype(jnp.float32), kv[:, :, 0].astype(jnp.float32)) * (HEAD_DIM ** -0.5)
    p = jax.nn.softmax(s, axis=-1)
    return jnp.einsum('bhtm,bmhe->bthe', p, kv[:, :, 1].astype(jnp.float32))


def _layer(x, pos, mem_kv, gla_s0, swa_bufs, g_ffn1, w_ffn1_in, w_ffn1_out, g_mix, w_in,
           w_gla_a2, b_gla_a, g_gla_out, w_gate, w_branch, w_out, g_ffn2, w_ffn2_in, w_ffn2_out):
    B, T, _ = x.shape
    h = x + 0.5 * _swiglu(_rmsnorm(x, g_ffn1), w_ffn1_in, w_ffn1_out)
    u = _rmsnorm(h, g_mix)
    z_swa, z_gq, z_gk, z_gv, z_gr, z_ga, z_mq = jnp.split(u @ w_in, SPLIT_POINTS, axis=-1)

    qkv = z_swa.reshape(B, T, 3, N_SWA, SWA_HEADS, HEAD_DIM)
    q = _rope(qkv[:, :, 0], pos)
    k = _rope(qkv[:, :, 1], pos)
    v = qkv[:, :, 2]
    outs, lses, rows = [], [], []
    for g, (win, dil) in enumerate(SWA_GROUPS):
        if swa_bufs is None:
            o_g, lse_g = _dilated_prompt(q[:, :, g], k[:, :, g], v[:, :, g], dil)
            keep = min(win, T)
            rows.append(jnp.stack([k[:, T - keep:, g], v[:, T - keep:, g]], axis=2))
        else:
            o_g, lse_g = _dilated_sample(q[:, :, g], k[:, :, g], v[:, :, g], swa_bufs[g], dil)
            rows.append(jnp.stack([k[:, :, g], v[:, :, g]], axis=2))
        outs.append(o_g)
        lses.append(lse_g)
    wgt = jax.nn.softmax(jnp.stack(lses, axis=0), axis=0)
    o_swa = jnp.sum(wgt[..., None] * jnp.stack(outs, axis=0), axis=0).reshape(B, T, BRANCH_W).astype(x.dtype)

    gq = z_gq.reshape(B, T, GLA_HEADS, GLA_DK) * (GLA_DK ** -0.5)
    gk = z_gk.reshape(B, T, GLA_HEADS, GLA_DK)
    gv = z_gv.reshape(B, T, GLA_HEADS, GLA_DV)
    gr = z_gr.reshape(B, T, GLA_HEADS, GLA_DV)
    log_a = jax.nn.log_sigmoid((z_ga @ w_gla_a2 + b_gla_a).astype(jnp.float32)) / GLA_TAU
    o_g, gla_state = _gla(gq, gk, gv, log_a.reshape(B, T, GLA_HEADS, GLA_DK), gla_s0)
    o_gla = (_rmsnorm(o_g, g_gla_out) * jax.nn.silu(gr.astype(jnp.float32))).reshape(B, T, BRANCH_W).astype(x.dtype)

    mq = z_mq.reshape(B, T, MEM_HEADS, HEAD_DIM)
    o_mem = _mem_attend(mq, mem_kv).reshape(B, T, BRANCH_W).astype(x.dtype)

    gates = jax.nn.sigmoid(u @ w_gate).reshape(B, T, N_BRANCH, D_MODEL)
    br = jnp.einsum('btnc,ncd->btnd', jnp.stack([o_swa, o_gla, o_mem], axis=2), w_branch)
    h = h + jnp.sum(gates * br, axis=2) @ w_out

    h = h + 0.5 * _swiglu(_rmsnorm(h, g_ffn2), w_ffn2_in, w_ffn2_out)
    return h, rows, gla_state


def setup_inputs(seed: int = 0) -> dict:
    key = jax.random.key(seed)
    ks = jax.random.split(key, 32)
    f32 = jnp.float32

    def nrm(k, shape, scale):
        return jax.random.normal(k, shape, f32) * scale

    def gain(k, n):
        return 1.0 + 0.02 * jax.random.normal(k, (DEPTH, n), f32)

    lens = [min(w, PAST_LEN) for w, _ in SWA_GROUPS]
    return {
        'x_prompt': nrm(ks[0], (BATCH, SEQ, D_MODEL), 1.0),
        'x_sample': nrm(ks[1], (DEC_BATCH, DEC_SEQ, D_MODEL), 1.0),
        'mem_prompt': nrm(ks[2], (BATCH, MEM_LEN, D_MODEL), 1.0),
        'cache_swa0': nrm(ks[3], (DEPTH, DEC_BATCH, lens[0], 2, SWA_HEADS, HEAD_DIM), 1.0),
        'cache_swa1': nrm(ks[4], (DEPTH, DEC_BATCH, lens[1], 2, SWA_HEADS, HEAD_DIM), 1.0),
        'cache_swa2': nrm(ks[5], (DEPTH, DEC_BATCH, lens[2], 2, SWA_HEADS, HEAD_DIM), 1.0),
        'cache_mem_kv': nrm(ks[6], (DEPTH, DEC_BATCH, MEM_LEN, 2, MEM_HEADS, HEAD_DIM), 1.0),
        'state_gla': nrm(ks[7], (DEPTH, DEC_BATCH, GLA_HEADS, GLA_DK, GLA_DV), 0.5),
        'g_ffn1': gain(ks[8], D_MODEL),
        'w_ffn1_in': nrm(ks[9], (DEPTH, D_MODEL, 2 * D_FF), D_MODEL ** -0.5),
        'w_ffn1_out': nrm(ks[10], (DEPTH, D_FF, D_MODEL), D_FF ** -0.5),
        'g_mix': gain(ks[11], D_MODEL),
        'w_in': nrm(ks[12], (DEPTH, D_MODEL, D_IN), D_MODEL ** -0.5),
        'w_gla_a2': nrm(ks[13], (DEPTH, GLA_RANK, GLA_HEADS * GLA_DK), GLA_RANK ** -0.5),
        'b_gla_a': nrm(ks[14], (DEPTH, GLA_HEADS * GLA_DK), 0.1),
        'g_gla_out': gain(ks[15], GLA_DV),
        'w_gate': nrm(ks[16], (DEPTH, D_MODEL, N_BRANCH * D_MODEL), D_MODEL ** -0.5),
        'w_branch': nrm(ks[17], (DEPTH, N_BRANCH, BRANCH_W, D_MODEL), BRANCH_W ** -0.5),
        'w_out': nrm(ks[18], (DEPTH, D_MODEL, D_MODEL), D_MODEL ** -0.5),
        'g_mem': gain(ks[19], D_MODEL),
        'w_mem_kv': nrm(ks[20], (DEPTH, D_MODEL, 2 * MEM_HEADS * HEAD_DIM), D_MODEL ** -0.5),
        'g_ffn2': gain(ks[21], D_MODEL),
        'w_ffn2_in': nrm(ks[22], (DEPTH, D_MODEL, 2 * D_FF), D_MODEL ** -0.5),
        'w_ffn2_out': nrm(ks[23], (DEPTH, D_FF, D_MODEL), D_FF ** -0.5),
        'g_final': 1.0 + 0.02 * jax.random.normal(ks[24], (D_MODEL,), f32),
    }


def reference(x_prompt, x_sample, mem_prompt, cache_swa0, cache_swa1, cache_swa2, cache_mem_kv, state_gla,
              g_ffn1, w_ffn1_in, w_ffn1_out, g_mix, w_in, w_gla_a2, b_gla_a, g_gla_out, w_gate, w_branch,
              w_out, g_mem, w_mem_kv, g_ffn2, w_ffn2_in, w_ffn2_out, g_final):
    pos_p = jnp.arange(x_prompt.shape[1], dtype=jnp.int32)
    pos_s = PAST_LEN + jnp.arange(x_sample.shape[1], dtype=jnp.int32)
    hp, hs = x_prompt, x_sample
    swa_p = [[] for _ in range(N_SWA)]
    swa_s = [[] for _ in range(N_SWA)]
    memkv_p, gla_p, gla_s = [], [], []
    for l in range(DEPTH):
        lw = (g_ffn1[l], w_ffn1_in[l], w_ffn1_out[l], g_mix[l], w_in[l], w_gla_a2[l], b_gla_a[l],
              g_gla_out[l], w_gate[l], w_branch[l], w_out[l], g_ffn2[l], w_ffn2_in[l], w_ffn2_out[l])
        kv_p = _mem_kv(mem_prompt, g_mem[l], w_mem_kv[l])
        s0 = jnp.zeros((x_prompt.shape[0], GLA_HEADS, GLA_DK, GLA_DV), jnp.float32)
        hp, rows_p, st_p = _layer(hp, pos_p, kv_p, s0, None, *lw)
        hs, rows_s, st_s = _layer(hs, pos_s, cache_mem_kv[l], state_gla[l],
                                  (cache_swa0[l], cache_swa1[l], cache_swa2[l]), *lw)
        for g in range(N_SWA):
            swa_p[g].append(rows_p[g])
            swa_s[g].append(rows_s[g])
        memkv_p.append(kv_p)
        gla_p.append(st_p)
        gla_s.append(st_s)
    y_prompt = _rmsnorm(hp, g_final)
    y_sample = _rmsnorm(hs, g_final)
    return (y_prompt, y_sample,
            jnp.stack(swa_p[0]), jnp.stack(swa_p[1]), jnp.stack(swa_p[2]), jnp.stack(memkv_p), jnp.stack(gla_p),
            jnp.stack(swa_s[0]), jnp.stack(swa_s[1]), jnp.stack(swa_s[2]), jnp.stack(gla_s))
```

```python
import contextlib
import numpy as np
import concourse.bass as bass
import concourse.mybir as mybir
from concourse.bass_utils import run_bass_kernel_spmd

F32 = mybir.dt.float32
BF16 = mybir.dt.bfloat16
AF = mybir.ActivationFunctionType
ALU = mybir.AluOpType
AX = mybir.AxisListType

PE, ACT, DVE, POOL, SP = "pe", "act", "dve", "pool", "sp"
ENGS = (PE, ACT, DVE, POOL, SP)


class Buf:
    __slots__ = ("name", "writers", "readers", "ld_sem", "ld_cnt", "st_sem", "st_cnt", "excl")

    def __init__(self, name, excl=False):
        self.name = name
        self.excl = excl
        self.writers = []
        self.readers = []
        self.ld_sem = None
        self.ld_cnt = 0
        self.st_sem = None
        self.st_cnt = 0


class Op:
    __slots__ = ("eng", "fn", "deps", "is_dma", "sem", "semval", "needed", "name")

    def __init__(self, eng, fn, is_dma=False, name=""):
        self.eng = eng
        self.fn = fn
        self.deps = []
        self.is_dma = is_dma
        self.sem = None
        self.semval = 0
        self.needed = False
        self.name = name


class Sched:
    def __init__(self, nc):
        self.nc = nc
        self.ops = {e: [] for e in ENGS}
        self.stack = contextlib.ExitStack()
        self.esem = {}
        self.dma_sems = []
        self.all_store_ops = []

    def new_sem(self, name):
        s = self.stack.enter_context(self.nc.semaphore(name))
        return s

    def _add_deps(self, op, reads, writes):
        deps = op.deps
        for b in reads:
            for w in b.writers:
                deps.append(w)
            if b.excl:
                for r in b.readers:
                    if r.eng != op.eng:
                        deps.append(r)
            b.readers.append(op)
        for b in writes:
            for w in b.writers:
                deps.append(w)
            for r in b.readers:
                if r is not op:
                    deps.append(r)
        for b in writes:
            b.writers = [op]
            b.readers = []

    def op(self, eng, fn, reads=(), writes=(), name=""):
        o = Op(eng, fn, name=name)
        self._add_deps(o, reads, writes)
        self.ops[eng].append(o)
        return o

    def dma(self, eng, fn, reads=(), writes=(), name="", accumulate_writers=False):
        o = Op(eng, fn, is_dma=True, name=name)
        if writes:
            b = writes[0]
            if b.ld_sem is None:
                b.ld_sem = {}
            if eng not in b.ld_sem:
                b.ld_sem[eng] = [self.new_sem("ld_" + eng + "_" + b.name), 0]
            b.ld_sem[eng][1] += 16
            o.sem, o.semval = b.ld_sem[eng][0], b.ld_sem[eng][1]
            self.all_store_ops.append(o)
        else:
            b = reads[0]
            if b.st_sem is None:
                b.st_sem = {}
            if eng not in b.st_sem:
                b.st_sem[eng] = [self.new_sem("st_" + eng + "_" + b.name), 0]
            b.st_sem[eng][1] += 16
            o.sem, o.semval = b.st_sem[eng][0], b.st_sem[eng][1]
            self.all_store_ops.append(o)
        if accumulate_writers and writes:
            prev = list(writes[0].writers)
            prev_r = list(writes[0].readers)
            if prev and all(p.is_dma and p.sem is o.sem for p in prev) and not prev_r:
                for b2 in reads:
                    for w in b2.writers:
                        o.deps.append(w)
                    b2.readers.append(o)
                for p in prev:
                    o.deps.extend(p.deps)
                writes[0].writers = prev + [o]
                self.ops[eng].append(o)
                return o
        self._add_deps(o, reads, writes)
        self.ops[eng].append(o)
        return o

    def emit(self, final_wait_eng=SP):
        nc = self.nc
        for e in ENGS:
            for o in self.ops[e]:
                for d in o.deps:
                    if d.is_dma:
                        continue
                    if d.eng == PE and o.eng == PE and not o.is_dma:
                        continue
                    d.needed = True
        for e in ENGS:
            self.esem[e] = self.new_sem("eng_" + e)
        for e in ENGS:
            cnt = 0
            for o in self.ops[e]:
                if o.is_dma:
                    continue
                if o.needed:
                    cnt += 1
                    o.sem, o.semval = self.esem[e], cnt
        final_waits = {}
        for o in self.all_store_ops:
            k = id(o.sem)
            if k not in final_waits or final_waits[k][1] < o.semval:
                final_waits[k] = (o.sem, o.semval)

        def emit_engine(eng_name, eng):
            waited = {}
            for o in self.ops[eng_name]:
                need = {}
                for d in o.deps:
                    if d.sem is None:
                        continue
                    if (not d.is_dma) and d.eng == PE and eng_name == PE and not o.is_dma:
                        continue
                    k = id(d.sem)
                    if waited.get(k, 0) >= d.semval:
                        continue
                    if k not in need or need[k][1] < d.semval:
                        need[k] = (d.sem, d.semval)
                for k, (s, v) in need.items():
                    eng.wait_ge(s, v)
                    waited[k] = v
                ins = o.fn(eng)
                if o.is_dma:
                    ins.then_inc(o.sem, 16)
                elif o.needed:
                    ins.then_inc(o.sem, 1)
            if eng_name == final_wait_eng:
                for k, (s, v) in final_waits.items():
                    if waited.get(k, 0) < v:
                        eng.wait_ge(s, v)

        with nc.Block() as block:
            @block.sync
            def _(e):
                emit_engine(SP, e)

            @block.scalar
            def _(e):
                emit_engine(ACT, e)

            @block.vector
            def _(e):
                emit_engine(DVE, e)

            @block.gpsimd
            def _(e):
                emit_engine(POOL, e)

            @block.tensor
            def _(e):
                emit_engine(PE, e)
        self.stack.close()

NCORE = 8
NT = 512
D = 1024
DFF = 2816
NJ = 22
DIN = 6672
SCALE = 128 ** -0.5
NEGB = -1.0e5
C_Q, C_K, C_V = 0, 1536, 3072
C_GQ, C_GK, C_GV, C_GR, C_GA, C_MQ = 4608, 4864, 5120, 5632, 6144, 6160


def _masks():
    k = np.arange(128)[:, None]
    q = np.arange(128)[None, :]
    own = np.where(k <= q, 0.0, NEGB)
    nxt = np.where(k >= q, 0.0, NEGB)
    kr, km = k % 4, k // 4
    qr, qm = q % 4, q // 4
    g2f = np.where(kr == qr, 0.0, NEGB)
    g2d = np.where((kr == qr) & (km <= qm), 0.0, NEGB)
    m = np.stack([np.tile(x, (1, 4)) for x in (own, nxt, g2f, g2d)]).astype(np.float32)
    sm = np.zeros((2, 128, 4), np.float32)
    sm[0] = np.where(k >= np.arange(4)[None, :], 0.0, NEGB)
    sm[1, :4] = np.where(np.arange(4)[:, None] <= np.arange(4)[None, :], 0.0, NEGB)
    return m, sm


def _gmasks():
    j = np.arange(128)[:, None]
    i = np.arange(128)[None, :]
    out = np.zeros((2, 2, 128, 128), np.float32)
    for n, c in enumerate((64, 4)):
        same = (j // c) == (i // c)
        out[n, 0] = (same & (j <= i)).astype(np.float32)
        out[n, 1] = (same & (j > i)).astype(np.float32)
    return out


def _rope_tabs():
    half = 64
    inv = 10000.0 ** (-np.arange(half, dtype=np.float64) / half)
    def tab(pos):
        ang = pos.astype(np.float64)[None, :] * inv[:, None]
        c = np.cos(ang.astype(np.float32).astype(np.float64))
        s = np.sin(ang.astype(np.float32).astype(np.float64))
        cos = np.concatenate([c, c], 0)
        sn = np.concatenate([s, -s], 0)
        return np.stack([cos, sn]).astype(np.float32)
    tp = tab(np.arange(2048))
    ts = tab(8192 + (np.arange(64) % 4))
    return tp, ts


class Prog:
    def __init__(self):
        nc = self.nc = bass.Bass("TRN2", target_bir_lowering=False)
        self.S = Sched(nc)
        self.uid = 0
        dt_in = lambda n, s, d=F32: nc.dram_tensor(n, s, d, kind="ExternalInput").ap()
        dt_out = lambda n, s: nc.dram_tensor(n, s, F32, kind="ExternalOutput").ap()
        I = self.I = {}
        I["xp"] = dt_in("xp", [4 * 2048, D])
        I["xs"] = dt_in("xs", [64, D])
        I["mem"] = dt_in("mem", [4 * 256, D])
        I["c0"] = dt_in("c0", [16, 128, 1024])
        I["c1"] = dt_in("c1", [16, 512, 1024])
        I["c2"] = dt_in("c2", [16, 2048, 1024])
        I["cm"] = dt_in("cm", [16, 256, 1024])
        I["sg"] = dt_in("sg", [16, 4, 64, 128])
        I["w1i"] = dt_in("w1i", [D, 2 * DFF])
        I["w1o"] = dt_in("w1o", [DFF, D])
        I["win"] = dt_in("win", [D, DIN])
        I["wg"] = dt_in("wg", [D, 3 * D])
        I["wb"] = dt_in("wb", [3 * 512, D])
        I["wo"] = dt_in("wo", [D, D])
        I["wm"] = dt_in("wm", [D, D])
        I["w2i"] = dt_in("w2i", [D, 2 * DFF])
        I["w2o"] = dt_in("w2o", [DFF, D])
        I["gains"] = dt_in("gains", [128, 4, 8])
        I["gfin"] = dt_in("gfin", [D])
        I["wa2"] = dt_in("wa2", [16, 256])
        I["ba"] = dt_in("ba", [1, 256])
        I["ggo"] = dt_in("ggo", [128, 1])
        I["masks"] = dt_in("masks", [4, 128, 512])
        I["smasks"] = dt_in("smasks", [2, 128, 4])
        I["gmask"] = dt_in("gmask", [2, 2, 128, 128])
        I["ident"] = dt_in("ident", [128, 128])
        I["ropep"] = dt_in("ropep", [2, 128, 2048])
        I["ropes"] = dt_in("ropes", [2, 128, 64])
        I["nmask"] = dt_in("nmask", [2, 64, 64])
        I["s0mask"] = dt_in("s0mask", [128, 16])
        I["onehot"] = dt_in("onehot", [64, 16])
        O = self.O = {}
        O["yp"] = dt_out("yp", [4 * 2048, D])
        O["ys"] = dt_out("ys", [64, D])
        O["s0p"] = dt_out("s0p", [4, 128, 2, 4, 128])
        O["s1p"] = dt_out("s1p", [4, 512, 2, 4, 128])
        O["s2p"] = dt_out("s2p", [4, 2048, 2, 4, 128])
        O["mkv"] = dt_out("mkv", [4 * 256, D])
        O["glap"] = dt_out("glap", [4, 4, 64, 128])
        O["s0s"] = dt_out("s0s", [64, 2, 4, 128])
        O["s1s"] = dt_out("s1s", [64, 2, 4, 128])
        O["s2s"] = dt_out("s2s", [64, 2, 4, 128])
        O["glas"] = dt_out("glas", [16, 4, 64, 128])
        W = self.W = {}
        self.WB = {}
        for n, shp in (("w1i", [D, 2 * DFF]), ("w1o", [DFF, D]), ("win", [D, DIN]), ("wg", [D, 3 * D]),
                       ("wb", [3 * 512, D]), ("wo", [D, D]), ("wm", [D, D]), ("w2i", [D, 2 * DFF]), ("w2o", [DFF, D])):
            W[n] = nc.dram_tensor(n + "16", shp, BF16).ap()
            self.WB[n] = Buf(n + "16")
        self.alloc()

    def sb(self, name, shape, dt):
        return self.nc.alloc_sbuf_tensor("sb_" + name, shape, dt)

    def nb(self, name):
        self.uid += 1
        return Buf(f"{name}_{self.uid}")

    def alloc(self):
        nc, sb = self.nc, self.sb
        self.banks = [nc.alloc_psum_tensor(f"bk{i}", [128, 512], F32) for i in range(8)]
        self.bbuf = [Buf(f"bk{i}", excl=True) for i in range(8)]
        self.pa, self.pb = 0, 0
        NS = 3
        self.ring = [sb(f"ring{i}", [128, 8, 528], BF16) for i in range(NS)]
        self.rbuf = [Buf(f"ring{i}") for i in range(NS)]
        self.plan = []
        self.issued = 0
        self.cur = 0
        self.xres = sb("xres", [128, 4, D], F32); self.Bxres = [Buf(f"xres{i}") for i in range(4)]
        self.xn = sb("xn", [128, 2, D], BF16); self.Bxn = [Buf(f"xn{i}") for i in range(2)]
        self.uT = sb("uT", [128, 8, NT], BF16); self.BuT = Buf("uT")
        self.sc = sb("sc", [128, NJ, NT], BF16); self.Bsc = [Buf(f"sc{i}") for i in range(NJ)]
        self.ss = sb("ss", [128, 8], F32); self.Bss = Buf("ss")
        self.rstd = sb("rstd", [128, 8], F32); self.Brstd = Buf("rstd")
        self.kT0 = sb("kT0", [128, 4, NT + 128], BF16); self.BkT0 = Buf("kT0"); self.BkT0p = Buf("kT0p")
        self.V0 = sb("V0", [128, 5, 512], BF16); self.BV0 = Buf("V0"); self.BV0p = Buf("V0p")
        self.kT1 = [sb(f"kT1_{i}", [128, 4, NT], BF16) for i in range(2)]; self.BkT1 = [Buf("kT1_0"), Buf("kT1_1")]
        self.V1 = [sb(f"V1_{i}", [128, 4, 512], BF16) for i in range(2)]; self.BV1 = [Buf("V1_0"), Buf("V1_1")]
        self.big = sb("big", [128, 16384], BF16)
        self.kT2 = self.big[:, 0:8192].rearrange("p (h n) -> p h n", h=4); self.BkT2 = [Buf(f"kT2_{i}") for i in range(4)]
        self.V2 = self.big[:, 8192:16384].rearrange("p (a n) -> p a n", a=16); self.BV2 = [Buf(f"V2_{i}") for i in range(4)]
        self.rope = sb("rope", [128, 2, NT], F32); self.Brope = Buf("rope")
        self.tA = [sb(f"tA{i}", [128, NT], F32) for i in range(2)]; self.BtA = [Buf("tA0"), Buf("tA1")]
        self.tB = [sb(f"tB{i}", [128, NT], F32) for i in range(2)]; self.BtB = [Buf("tB0"), Buf("tB1")]
        self.pT = [sb(f"pT{i}", [128, NT], BF16) for i in range(3)]; self.BpT = [Buf(f"pT{i}") for i in range(3)]
        self.ipT = 0
        self.stg = [sb(f"stg{i}", [128, 512], F32) for i in range(3)]; self.Bstg = [Buf(f"stg{i}") for i in range(3)]
        self.istg = 0
        self.osum = sb("osum", [128, NT], F32); self.Bosum = Buf("osum")
        self.dsum = sb("dsum", [128, NT], F32); self.Bdsum = Buf("dsum")
        self.KmT = sb("KmT", [128, 4, 256], BF16); self.BKmT = Buf("KmT")
        self.Vm = sb("Vm", [128, 2, 512], BF16); self.BVm = Buf("Vm")
        self.gkv = sb("gkv", [128, 4, 768], BF16); self.Bgkv = [Buf(f"gkv{i}") for i in range(4)]
        self.gl = sb("gl", [128, 1, 256], F32); self.Bgl = [Buf("gl0")] * 4
        self.gqT = sb("gqT", [128, 2, NT], BF16); self.BgqT = Buf("gqT")
        self.gkT = sb("gkT", [128, 2, NT], BF16); self.BgkT = Buf("gkT")
        self.gaT = sb("gaT", [16, NT], BF16); self.BgaT = Buf("gaT")
        self.qeTm = sb("qeTm", [128, 2, 2, 128], BF16); self.BqeT = Buf("qeTm")
        self.keT = sb("keT", [128, 2, 128], BF16); self.BkeT = Buf("keT")
        self.eb = sb("eb", [128, 2, 128], F32); self.Beb = Buf("eb")
        self.enb = sb("enb", [128, 2, 128], F32); self.Benb = Buf("enb")
        self.ekd = sb("ekd", [128, 256], F32); self.Bekd = Buf("ekd")
        self.kdm = sb("kdm", [128, 2, 256], BF16); self.Bkd = Buf("kdm")
        self.attT = sb("attT", [128, 4, 128], BF16); self.BattT = Buf("attT")
        self.S32 = sb("S32", [128, 2, 256], F32); self.BS32 = [Buf("S32_0"), Buf("S32_1")]
        self.Sbf = [sb(f"Sbf{i}", [128, 2, 256], BF16) for i in range(2)]; self.BSbf = [[Buf(f"Sbf{i}_{p}") for p in range(2)] for i in range(2)]
        self.decay = sb("decay", [128, 2, 2], F32); self.Bdecay = Buf("decay")
        self.gains = sb("gains", [128, 4, 8], F32); self.Bconst = Buf("const")
        self.gfin = sb("gfinb", [128, D], F32)
        self.wa2 = sb("wa2", [16, 256], BF16)
        self.ba = sb("ba", [1, 256], BF16)
        self.ggo = sb("ggo", [128, 1], F32)
        self.masks = sb("masksb", [128, 4, 512], BF16)
        self.gmask = sb("gmask", [128, 4, 128], F32)
        self.identb = sb("identb", [128, 128], BF16)
        self.identf = sb("identf", [128, 128], F32)
        self.onesb = sb("onesb", [128, 128], BF16)
        self.ropes = sb("ropes", [128, 2, 64], F32)

    def bankA(self):
        i = self.pa % 4
        self.pa += 1
        return self.banks[i], self.bbuf[i]

    def bankB(self):
        i = 4 + self.pb % 4
        self.pb += 1
        return self.banks[i], self.bbuf[i]

    def mm(self, out, lhsT, rhs, start, reads, writes):
        self.S.op(PE, lambda e: e.matmul(out, lhsT=lhsT, rhs=rhs, start=start, stop=True, skip_group_check=True),
                  reads=reads, writes=writes)

    def tr(self, out, in_, ident, reads, writes):
        self.S.op(PE, lambda e: e.transpose(out=out, in_=in_, identity=ident), reads=reads + [self.Bconst], writes=writes)

    def act(self, out, in_, func, reads, writes, **kw):
        self.S.op(ACT, lambda e: e.activation(out=out, in_=in_, func=func, **kw), reads=reads, writes=writes)

    def tt(self, out, in0, in1, op, reads, writes, eng=DVE):
        self.S.op(eng, lambda e: e.tensor_tensor(out=out, in0=in0, in1=in1, op=op), reads=reads, writes=writes)

    def stt(self, out, in0, scalar, in1, op0, op1, reads, writes, eng=DVE):
        self.S.op(eng, lambda e: e.scalar_tensor_tensor(out=out, in0=in0, scalar=scalar, in1=in1, op0=op0, op1=op1),
                  reads=reads, writes=writes)

    def cp(self, out, in_, reads, writes, eng=DVE):
        self.S.op(eng, lambda e: e.tensor_copy(out=out, in_=in_), reads=reads, writes=writes)

    def ld(self, out, in_, writes, reads=(), eng=POOL, acc=False):
        self.S.dma(eng, lambda e: e.dma_start(out=out, in_=in_), reads=list(reads), writes=list(writes), accumulate_writers=acc)

    def st(self, out, in_, reads, eng=POOL):
        self.S.dma(eng, lambda e: e.dma_start(out=out, in_=in_), reads=list(reads), writes=[])

    def next_pT(self):
        i = self.ipT % 3
        self.ipT += 1
        return self.pT[i], self.BpT[i]

    def next_stg(self):
        i = self.istg % 3
        self.istg += 1
        return self.stg[i], self.Bstg[i]

    def unit(self):
        n = len(self.ring)
        while self.issued < min(len(self.plan), self.cur + n - 1):
            u = self.plan[self.issued]
            slot, sbuf = self.ring[self.issued % n], self.rbuf[self.issued % n]
            for k, (dst_fn, src, wname) in enumerate(u):
                self.ld(dst_fn(slot), src, [sbuf], reads=[self.WB[wname]], eng=SP, acc=(k > 0))
            self.issued += 1
        i = self.cur
        self.cur += 1
        return self.ring[i % n], self.rbuf[i % n]

    def plan_ffn(self, wi, wo):
        W = self.W
        wiv = W[wi].rearrange("(k p) n -> p k n", p=128)
        wov = W[wo].rearrange("(j p) n -> p j n", p=128)
        us = []
        for u in range(11):
            us.append([(lambda s: s[:, :, 0:256], wiv[:, :, 256 * u:256 * u + 256], wi),
                       (lambda s: s[:, :, 256:512], wiv[:, :, DFF + 256 * u:DFF + 256 * u + 256], wi)])
        for dh in range(2):
            for j0 in (0, 8, 16):
                nj = min(8, NJ - j0)
                us.append([(lambda s, nj=nj: s[:, 0:nj, 0:512], wov[:, j0:j0 + nj, dh * 512:(dh + 1) * 512], wo)])
        return us

    def plan_win(self):
        wv = self.W["win"].rearrange("(k p) n -> p k n", p=128)
        us = []
        for u in range(13):
            c0 = 512 * u
            n = 512 if u < 12 else 528
            us.append([(lambda s, n=n: s[:, :, 0:n], wv[:, :, c0:c0 + n], "win")])
        return us

    def plan_merge(self):
        gv = self.W["wg"].rearrange("(k p) n -> p k n", p=128)
        bv = self.W["wb"].rearrange("(c p) n -> p c n", p=128)
        ov = self.W["wo"].rearrange("(k p) n -> p k n", p=128)
        us = []
        for half in range(2):
            for b in range(3):
                us.append([(lambda s: s[:, :, 0:512], gv[:, :, b * 1024 + half * 512:b * 1024 + half * 512 + 512], "wg")])
                us.append([(lambda s: s[:, 0:4, 0:512], bv[:, 4 * b:4 * b + 4, half * 512:half * 512 + 512], "wb")])
        for dh in range(2):
            us.append([(lambda s: s[:, :, 0:512], ov[:, :, dh * 512:(dh + 1) * 512], "wo")])
        return us

    def plan_memkv(self):
        mv = self.W["wm"].rearrange("(k p) n -> p k n", p=128)
        return [[(lambda s: s[:, :, 0:512], mv[:, :, half * 512:(half + 1) * 512], "wm")] for half in range(2)]

    def plan_tile(self):
        return self.plan_ffn("w1i", "w1o") + self.plan_win() + self.plan_merge() + self.plan_ffn("w2i", "w2o")

    def prologue(self):
        I, W = self.I, self.W
        for n in ("wm", "w1i", "w1o", "win", "wg", "wb", "wo", "w2i", "w2o"):
            self.ld(W[n], I[n], [self.WB[n]], eng=POOL)
        B = self.Bconst
        ld = lambda out, in_, k: self.ld(out, in_, [B], eng=SP, acc=(k > 0))
        ld(self.gains[:], I["gains"], 0)
        ld(self.gfin[:], I["gfin"].partition_broadcast(128), 1)
        ld(self.ggo[:], I["ggo"], 1)
        ld(self.gmask[:], I["gmask"].rearrange("a b p n -> p (a b) n"), 1)
        ld(self.identf[:], I["ident"], 1)
        ld(self.ropes[:], I["ropes"].rearrange("c p n -> p c n"), 1)
        Bt = self.Bxres
        ct = self.xres
        self.ld(ct[:, :, 0:512], I["masks"].rearrange("m p n -> p m n"), Bt, eng=SP)
        self.cp(self.masks[:], ct[:, :, 0:512], Bt, [B])
        self.ld(ct[0:16, 0, 0:256], I["wa2"], Bt, eng=SP)
        self.cp(self.wa2[:], ct[0:16, 0, 0:256], Bt, [B])
        self.ld(ct[0:1, 1, 0:256], I["ba"], Bt, eng=SP)
        self.cp(self.ba[:], ct[0:1, 1, 0:256], Bt, [B])
        self.cp(self.identb[:], self.identf[:], [B], [B])
        self.S.op(DVE, lambda e: e.memset(self.onesb[:], 1.0), reads=[], writes=[B])
        self.S.op(DVE, lambda e: e.memset(self.qeTm[:], 0.0), reads=[], writes=[self.BqeT])
        self.S.op(DVE, lambda e: e.memset(self.kdm[:], 0.0), reads=[], writes=[self.Bkd])

    def norm_uT(self, gidx, blocks):
        nb = len(blocks)
        for tb, sz in blocks:
            self.act(self.xn[:sz, tb % 2, :], self.xres[:sz, tb, :], AF.Square, [self.Bxres[tb]], [self.Bxn[tb % 2], self.Bss],
                     accum_out=self.ss[:sz, tb:tb + 1])
        sz = blocks[0][1]
        stage = getattr(self, "stage", 9)
        if stage <= 1:
            return
        self.act(self.rstd[:sz, 0:nb], self.ss[:sz, 0:nb], AF.Ln, [self.Bss], [self.Brstd], scale=1.0 / D, bias=1e-6)
        self.act(self.rstd[:sz, 0:nb], self.rstd[:sz, 0:nb], AF.Exp, [self.Brstd], [self.Brstd], scale=-0.5)
        for tb, sz in blocks:
            self.act(self.xn[:sz, tb % 2, :], self.xres[:sz, tb, :], AF.Copy, [self.Bxres[tb], self.Brstd], [self.Bxn[tb % 2]],
                     scale=self.rstd[:sz, tb:tb + 1])
            if stage <= 2:
                continue
            bk, bb = self.bankA()
            bkb = bk.bitcast(BF16)
            for k in range(8):
                self.tr(bkb[:, k * 128:k * 128 + sz], self.xn[:sz, tb % 2, k * 128:(k + 1) * 128], self.identb[:sz, :sz],
                        [self.Bxn[tb % 2]], [bb])
            self.tt(self.uT[:, :, tb * 128:tb * 128 + sz],
                    bkb[:, :].rearrange("p (k n) -> p k n", k=8)[:, :, 0:sz],
                    self.gains[:, gidx, :].unsqueeze(2).to_broadcast([128, 8, sz]), ALU.mult,
                    [bb, self.Bconst], [self.BuT])

    def ffn(self, blocks, ntok):
        B = self.Bsc
        for u in range(11):
            slot, sbuf = self.unit()
            for jj in range(2):
                j = 2 * u + jj
                ba_, bba = self.bankA()
                for k in range(8):
                    self.mm(ba_[:, 0:ntok], slot[:, k, jj * 128:(jj + 1) * 128], self.uT[:, k, 0:ntok], k == 0, [sbuf, self.BuT], [bba])
                bb_, bbb = self.bankA()
                for k in range(8):
                    self.mm(bb_[:, 0:ntok], slot[:, k, 256 + jj * 128:256 + (jj + 1) * 128], self.uT[:, k, 0:ntok], k == 0, [sbuf, self.BuT], [bbb])
                t, tb_ = self.next_pT()
                self.act(t[:, 0:ntok], ba_[:, 0:ntok], AF.Silu, [bba], [tb_])
                self.tt(self.sc[:, j, 0:ntok], bb_[:, 0:ntok], t[:, 0:ntok], ALU.mult, [bbb, tb_], [B[j]])
        for dh in range(2):
            getb = self.bankA if dh == 0 else self.bankB
            obk = [getb() for _ in blocks]
            for j0 in (0, 8, 16):
                slot, sbuf = self.unit()
                for jj in range(min(8, NJ - j0)):
                    j = j0 + jj
                    for bi, (tb, sz) in enumerate(blocks):
                        self.mm(obk[bi][0][:sz, :], self.sc[:, j, tb * 128:tb * 128 + sz], slot[:, jj, 0:512], j == 0,
                                [sbuf, B[j]], [obk[bi][1]])
            for bi, (tb, sz) in enumerate(blocks):
                xs_ = self.xres[:sz, tb, dh * 512:(dh + 1) * 512]
                self.stt(xs_, obk[bi][0][:sz, :], 0.5, xs_, ALU.mult, ALU.add, [obk[bi][1], self.Bxres[tb]], [self.Bxres[tb]])

    def up_chunk(self, slot, sbuf, c, m, ntok):
        bk, bb = self.bankA()
        for k in range(8):
            self.mm(bk[0:m, 0:ntok], slot[:, k, c:c + m], self.uT[:, k, 0:ntok], k == 0, [sbuf, self.BuT], [bb])
        return bk, bb

    def rope_evac(self, bk, bb, ntok, cosT, snT, Brope, dst_fn, Bdst, fp32_out=None):
        self.irope = getattr(self, "irope", 0) + 1
        i = self.irope % 2
        tA, BA, tB, BB = self.tA[i], self.BtA[i], self.tB[i], self.BtB[i]
        self.tt(tA[:, 0:ntok], bk[:, 0:ntok], cosT, ALU.mult, [bb, Brope], [BA])
        self.tt(tB[0:64, 0:ntok], bk[64:128, 0:ntok], snT[64:128], ALU.mult, [bb, Brope], [BB])
        self.tt(tB[64:128, 0:ntok], bk[0:64, 0:ntok], snT[0:64], ALU.mult, [bb, Brope], [BB])
        if fp32_out is None:
            self.tt(dst_fn(None), dst_fn(tA[:, 0:ntok]), dst_fn(tB[:, 0:ntok]), ALU.add, [BA, BB], Bdst, eng=POOL)
        else:
            self.tt(tA[:, 0:ntok], tA[:, 0:ntok], tB[:, 0:ntok], ALU.add, [BA, BB], [BA], eng=POOL)
            self.act(dst_fn(None), dst_fn(tA[:, 0:ntok]), AF.Copy, [BA], Bdst)
        return tA, BA

    def out_krows(self, tK, BK, dram_rows_h):
        bk, bb = self.bankA()
        for b in range(4):
            self.tr(bk[:, b * 128:(b + 1) * 128], tK[:, b * 128:(b + 1) * 128], self.identf[:], [BK], [bb])
        stg, Bs = self.next_stg()
        self.cp(stg[:, 0:512], bk[:, :], [bb], [Bs])
        self.st(dram_rows_h, stg[:, 0:512].rearrange("p (b e) -> p b e", b=4), [Bs])

    def prompt_tile(self, s, i):
        I, O = self.I, self.O
        blocks = [(tb, 128) for tb in range(4)]
        row0 = s * 2048 + i * NT
        par = i % 2
        self.ld(self.xres[:], I["xp"][row0:row0 + NT, :].rearrange("(b p) d -> p b d", p=128), self.Bxres)
        self.norm_uT(0, blocks)
        self.ffn(blocks, NT)
        self.norm_uT(1, blocks)
        self.ld(self.rope[:], I["ropep"][:, :, i * NT:(i + 1) * NT].rearrange("c p n -> p c n"), [self.Brope])
        cosT, snT = self.rope[:, 0, :], self.rope[:, 1, :]
        sc, Bsc = self.sc, self.Bsc
        last = (i == 3)
        def qdst(g, h):
            return sc[:, g * 4 + h, :], [Bsc[g * 4 + h]]
        def kdst(g, h):
            if g == 0:
                return self.kT0[:, h, 128:128 + NT], [self.BkT0]
            if g == 1:
                return self.kT1[par][:, h, :], [self.BkT1[par]]
            return self.kT2[:, h, i * NT:(i + 1) * NT], [self.BkT2[i]]
        def perm(g, dst):
            def f(src):
                a = dst if src is None else src
                if g == 0:
                    return a
                if src is None:
                    return a.rearrange("p (r m) -> p m r", r=4)
                return a.rearrange("p (m r) -> p m r", r=4)
            return f
        for u in range(13):
            slot, sbuf = self.unit()
            ncol = 512 if u < 12 else 528
            base = u * 512
            c = 0
            while c < ncol:
                col = base + c
                if col < C_V:
                    isk = col >= C_K
                    gh = (col - (C_K if isk else 0)) // 128
                    g, h = gh // 4, gh % 4
                    bk, bb = self.up_chunk(slot, sbuf, c, 128, NT)
                    dst, Bd = kdst(g, h) if isk else qdst(g, h)
                    need_rows = isk and (g == 2 or last)
                    if need_rows:
                        tKx, BKx = self.rope_evac(bk, bb, NT, cosT, snT, self.Brope, perm(g, dst), Bd, fp32_out=True)
                        keep = (128, 512, 2048)[g]
                        od = O[("s0p", "s1p", "s2p")[g]]
                        if g == 2:
                            self.out_krows(tKx, BKx,
                                           od[s, i * NT:(i + 1) * NT, 0, h, :].rearrange("(b p) e -> p b e", p=128))
                        elif g == 1:
                            self.out_krows(tKx, BKx, od[s, :, 0, h, :].rearrange("(b p) e -> p b e", p=128))
                        else:
                            bk2, bb2 = self.bankA()
                            self.tr(bk2[:, 0:128], tKx[:, 384:512], self.identf[:], [BKx], [bb2])
                            stg, Bs = self.next_stg()
                            self.cp(stg[:, 0:128], bk2[:, 0:128], [bb2], [Bs])
                            self.st(od[s, :, 0, h, :], stg[:, 0:128], [Bs])
                    else:
                        self.rope_evac(bk, bb, NT, cosT, snT, self.Brope, perm(g, dst), Bd)
                    c += 128
                elif col < C_GQ:
                    g = (col - C_V) // 512
                    for blk in range(4):
                        if g == 0:
                            sel = lambda k: self.uT[:, k, blk * 128:(blk + 1) * 128]
                        else:
                            sel = lambda k: self.uT[:, k, :].rearrange("p (m r) -> p r m", r=4)[:, blk, :]
                        bk, bb = self.bankA()
                        for k in range(8):
                            self.mm(bk[:, :], sel(k), slot[:, k, c:c + 512], k == 0, [sbuf, self.BuT], [bb])
                        if g == 0:
                            vdst, Bv = self.V0[:, 1 + blk, :], [self.BV0]
                        elif g == 1:
                            vdst, Bv = self.V1[par][:, blk, :], [self.BV1[par]]
                        else:
                            vdst, Bv = self.V2[:, i * 4 + blk, :], [self.BV2[i]]
                        self.act(vdst, bk[:, :], AF.Copy, [bb], Bv)
                        need = (g == 2) or (last and (g == 1 or blk == 3))
                        if need:
                            stg, Bs = self.next_stg()
                            self.cp(stg[:, 0:512], bk[:, :], [bb], [Bs])
                            if g == 0:
                                self.st(O["s0p"][s, :, 1, :, :].rearrange("p h e -> p (h e)"), stg[:, 0:512], [Bs])
                            elif g == 1:
                                self.st(O["s1p"][s, :, 1, :, :].rearrange("(m r) h e -> r m (h e)", r=4)[blk], stg[:, 0:512], [Bs])
                            else:
                                self.st(O["s2p"][s, i * NT:(i + 1) * NT, 1, :, :].rearrange("(m r) h e -> r m (h e)", r=4)[blk], stg[:, 0:512], [Bs])
                    c += 512
                elif col < C_GK:
                    p = (col - C_GQ) // 128
                    bk, bb = self.up_chunk(slot, sbuf, c, 128, NT)
                    self.cp(self.gqT[:, p, :], bk[:, :], [bb], [self.BgqT])
                    c += 128
                elif col < C_GV:
                    p = (col - C_GK) // 128
                    bk, bb = self.up_chunk(slot, sbuf, c, 128, NT)
                    self.cp(self.gkT[:, p, :], bk[:, :], [bb], [self.BgkT])
                    for blk in range(4):
                        bk, bb = self.bankA()
                        for k in range(8):
                            self.mm(bk[:, 0:128], self.uT[:, k, blk * 128:(blk + 1) * 128], slot[:, k, c:c + 128], k == 0, [sbuf, self.BuT], [bb])
                        self.act(self.gkv[:, blk, p * 128:(p + 1) * 128], bk[:, 0:128], AF.Copy, [bb], [self.Bgkv[blk]])
                    c += 128
                elif col < C_GR:
                    for blk in range(4):
                        bk, bb = self.bankA()
                        for k in range(8):
                            self.mm(bk[:, :], self.uT[:, k, blk * 128:(blk + 1) * 128], slot[:, k, c:c + 512], k == 0, [sbuf, self.BuT], [bb])
                        self.act(self.gkv[:, blk, 256:768], bk[:, :], AF.Copy, [bb], [self.Bgkv[blk]])
                    c += 512
                elif col < C_GA:
                    h = (col - C_GR) // 128
                    bk, bb = self.up_chunk(slot, sbuf, c, 128, NT)
                    self.act(sc[:, 16 + h, :], bk[:, :], AF.Silu, [bb], [Bsc[16 + h]])
                    c += 128
                elif col < C_MQ:
                    bk, bb = self.up_chunk(slot, sbuf, c, 16, NT)
                    self.cp(self.gaT[:, :], bk[0:16, :], [bb], [self.BgaT])
                    c += 16
                else:
                    h = (col - C_MQ) // 128
                    bk, bb = self.up_chunk(slot, sbuf, c, 128, NT)
                    self.act(sc[:, 12 + h, :], bk[:, :], AF.Copy, [bb], [Bsc[12 + h]])
                    c += 128
        self.swa_attention(i, par)
        self.gla_prompt(s, i)
        self.mem_attention()
        self.merge(blocks, NT)
        self.cp(self.kT0[:, :, 0:128], self.kT0[:, :, NT:NT + 128], [self.BkT0], [self.BkT0p], eng=POOL)
        self.cp(self.V0[:, 0, :], self.V0[:, 4, :], [self.BV0], [self.BV0p], eng=POOL)
        if getattr(self, "stop_after_merge", False):
            return
        self.norm_uT(2, blocks)
        self.ffn(blocks, NT)
        self.final_norm(blocks, O["yp"][row0:row0 + NT, :])

    def score_block(self, kT_blk, Bk, q_ap, Bq, ncol, mask_ap):
        bk, bb = self.bankA()
        self.mm(bk[:, 0:ncol], kT_blk, q_ap, True, Bk + Bq, [bb])
        self.mm(bk[:, 0:ncol], self.identb[:], mask_ap, False, [self.Bconst], [bb])
        pt, Bp = self.next_pT()
        self.act(pt[:, 0:ncol], bk[:, 0:ncol], AF.Exp, [bb], [Bp], scale=SCALE)
        return pt, Bp

    def swa_attention(self, i, par):
        sc, Bsc = self.sc, self.Bsc
        M = self.masks
        for h in range(4):
            first_sum = True
            for g in range(3):
                ob, Bo = self.bankB()
                db, Bd = self.bankB()
                first = [True]
                def pv(pt, Bp, vblk, Bv, c0, ncol, dcols=None):
                    self.mm(ob[:, c0:c0 + ncol], vblk, pt[:, 0:ncol], first[0], Bv + [Bp], [Bo])
                    first[0] = False
                def dn(pt, Bp, c0, ncol, fst):
                    self.mm(db[:, c0:c0 + ncol], self.onesb[:], pt[:, 0:ncol], fst, [Bp, self.Bconst], [Bd])
                q_all = sc[:, g * 4 + h, :]
                Bq = [Bsc[g * 4 + h]]
                dfirst = True
                if g == 0:
                    kbs = ([-1] if i > 0 else []) + [0, 1, 2, 3]
                    for kb in kbs:
                        if kb < 0:
                            kblk, Bk = self.kT0[:, h, 0:128], [self.BkT0p]
                            vblk, Bv = self.V0[:, 0, h * 128:(h + 1) * 128], [self.BV0p]
                            c0, ncol, mk = 0, 128, M[:, 1, 0:128]
                        else:
                            kblk, Bk = self.kT0[:, h, 128 + kb * 128:256 + kb * 128], [self.BkT0]
                            vblk, Bv = self.V0[:, 1 + kb, h * 128:(h + 1) * 128], [self.BV0]
                            c0 = kb * 128
                            ncol = 256 if kb < 3 else 128
                            mk = M[:, 0, 0:128] if ncol == 128 else None
                        if mk is None:
                            pt, Bp = self.score_block2(kblk, Bk, q_all[:, c0:c0 + 256], Bq)
                        else:
                            pt, Bp = self.score_block(kblk, Bk, q_all[:, c0:c0 + ncol], Bq, ncol, mk)
                        pv(pt, Bp, vblk, Bv, c0, ncol)
                        dn(pt, Bp, c0, ncol, dfirst)
                        dfirst = False
                else:
                    if g == 1:
                        sets = ([(self.kT1[1 - par], [self.BkT1[1 - par]], self.V1[1 - par], [self.BV1[1 - par]], 0, 1)] if i > 0 else [])
                        sets.append((self.kT1[par], [self.BkT1[par]], self.V1[par], [self.BV1[par]], 0, 0))
                    else:
                        sets = []
                        for ip in range(i + 1):
                            sets.append((self.kT2, [self.BkT2[ip]], self.V2, [self.BV2[ip]], ip, 2 if ip < i else 3))
                    for (kt, Bk, vt, Bv, ip, mtype) in sets:
                        bk, bb = self.bankA()
                        for r in range(4):
                            kblk = kt[:, h, ip * NT + r * 128:ip * NT + (r + 1) * 128]
                            self.mm(bk[:, r * 128:(r + 1) * 128], kblk, q_all[:, r * 128:(r + 1) * 128], r == 0, Bk + Bq, [bb])
                        self.mm(bk[:, :], self.identb[:], M[:, mtype, :], False, [self.Bconst], [bb])
                        pt, Bp = self.next_pT()
                        self.act(pt[:, :], bk[:, :], AF.Exp, [bb], [Bp], scale=SCALE)
                        for r in range(4):
                            vblk = vt[:, (ip * 4 if g == 2 else 0) + r, h * 128:(h + 1) * 128]
                            self.mm(ob[:, r * 128:(r + 1) * 128], vblk, pt[:, r * 128:(r + 1) * 128], first[0], Bv + [Bp], [Bo])
                            first[0] = False
                        dn(pt, Bp, 0, NT, dfirst)
                        dfirst = False
                if g == 0:
                    nat = lambda a: a
                else:
                    nat = lambda a: a.rearrange("p (r m) -> p m r", r=4)
                osum = self.osum[:, :]
                dsum = self.dsum[:, :]
                shp = (lambda a: a) if g == 0 else (lambda a: a.rearrange("p (m r) -> p m r", r=4))
                if first_sum:
                    self.cp(osum, ob[:, :], [Bo], [self.Bosum])
                    self.act(dsum, db[:, :], AF.Copy, [Bd], [self.Bdsum])
                    first_sum = False
                else:
                    self.tt(shp(osum), nat(ob[:, :]), shp(osum), ALU.add, [Bo, self.Bosum], [self.Bosum])
                    self.tt(shp(dsum), nat(db[:, :]), shp(dsum), ALU.add, [Bd, self.Bdsum], [self.Bdsum])
            self.S.op(DVE, lambda e: e.reciprocal(out=self.dsum[:, :], in_=self.dsum[:, :]), reads=[self.Bdsum], writes=[self.Bdsum])
            self.tt(sc[:, h, :], self.osum[:, :], self.dsum[:, :], ALU.mult, [self.Bosum, self.Bdsum], [Bsc[h]], eng=POOL)

    def score_block2(self, kblk, Bk, q_ap, Bq):
        bk, bb = self.bankA()
        self.mm(bk[:, 0:256], kblk, q_ap, True, Bk + Bq, [bb])
        self.mm(bk[:, 0:128], self.identb[:], self.masks[:, 0, 0:128], False, [self.Bconst], [bb])
        self.mm(bk[:, 128:256], self.identb[:], self.masks[:, 1, 0:128], False, [self.Bconst], [bb])
        pt, Bp = self.next_pT()
        self.act(pt[:, 0:256], bk[:, 0:256], AF.Exp, [bb], [Bp], scale=SCALE)
        return pt, Bp

    def mem_attention(self, ntok=NT, per_seq=None):
        sc, Bsc = self.sc, self.Bsc
        for h in range(4):
            ob, Bo = self.bankB()
            db, Bd = self.bankB()
            for mb in range(2):
                bk, bb = self.bankA()
                self.mm(bk[:, 0:ntok], self.KmT[:, h, mb * 128:(mb + 1) * 128], sc[:, 12 + h, 0:ntok], True, [self.BKmT, Bsc[12 + h]], [bb])
                pt, Bp = self.next_pT()
                self.act(pt[:, 0:ntok], bk[:, 0:ntok], AF.Exp, [bb], [Bp], scale=SCALE)
                self.mm(ob[:, 0:ntok], self.Vm[:, mb, h * 128:(h + 1) * 128], pt[:, 0:ntok], mb == 0, [self.BVm, Bp], [Bo])
                self.mm(db[:, 0:ntok], self.onesb[:], pt[:, 0:ntok], mb == 0, [Bp, self.Bconst], [Bd])
            self.S.op(DVE, lambda e, db=db: e.reciprocal(out=self.dsum[:, 0:ntok], in_=db[:, 0:ntok]), reads=[Bd], writes=[self.Bdsum])
            self.tt(sc[:, 12 + h, 0:ntok], ob[:, 0:ntok], self.dsum[:, 0:ntok], ALU.mult, [Bo, self.Bdsum], [Bsc[12 + h]])

    def gla_block_prep(self, blk, sz, mi):
        gm = self.gmask
        tri, gt = gm[:sz, 2 * mi, 0:sz], gm[:sz, 2 * mi + 1, 0:sz]
        c0 = blk * 128
        bk, bb = self.bankA()
        self.mm(bk[:sz, 0:256], self.gaT[:, c0:c0 + sz], self.wa2[:, :], True, [self.BgaT, self.Bconst], [bb])
        self.mm(bk[:sz, 0:256], self.onesb[0:1, 0:sz], self.ba[:, :], False, [self.Bconst], [bb])
        self.act(self.gl[:sz, 0, :], bk[:sz, 0:256], AF.Exp, [bb], [self.Bgl[blk]], scale=-1.0)
        self.act(self.gl[:sz, 0, :], self.gl[:sz, 0, :], AF.Ln, [self.Bgl[blk]], [self.Bgl[blk]], bias=1.0)
        bkb, bbb = self.bankA()
        for p in range(2):
            self.mm(bkb[:, p * 128:p * 128 + sz], self.gl[:sz, 0, p * 128:(p + 1) * 128], tri, True, [self.Bgl[blk], self.Bconst], [bbb])
        bkc, bbc = self.bankA()
        self.mm(bkc[:sz, 0:256], gt, self.gl[:sz, 0, :], True, [self.Bgl[blk], self.Bconst], [bbc])
        for p in range(2):
            self.act(self.eb[:, p, 0:sz], bkb[:, p * 128:p * 128 + sz], AF.Exp, [bbb], [self.Beb], scale=-1.0 / 16)
            self.act(self.enb[:, p, 0:sz], bkb[:, p * 128:p * 128 + sz], AF.Exp, [bbb], [self.Benb], scale=1.0 / 16)
        self.act(self.ekd[:sz, :], bkc[:sz, 0:256], AF.Exp, [bbc], [self.Bekd], scale=-1.0 / 16)
        for p in range(2):
            for hl in range(2):
                r = slice(64 * hl, 64 * hl + 64)
                self.stt(self.qeTm[r, hl, p, 0:sz], self.gqT[r, p, c0:c0 + sz], 0.125, self.eb[r, p, 0:sz], ALU.mult, ALU.mult,
                         [self.BgqT, self.Beb], [self.BqeT])
            self.tt(self.keT[:, p, 0:sz], self.gkT[:, p, c0:c0 + sz], self.enb[:, p, 0:sz], ALU.mult, [self.BgkT, self.Benb], [self.BkeT])
        for c in range(sz // 64):
            r = slice(64 * c, 64 * c + 64)
            self.tt(self.kdm[r, c, :], self.gkv[r, blk, 0:256], self.ekd[r, :], ALU.mult, [self.Bgkv[blk], self.Bekd], [self.Bkd])
        for h in range(4):
            p, hl = h // 2, h % 2
            bka, bba = self.bankA()
            self.mm(bka[:sz, 0:sz], self.keT[:, p, 0:sz], self.qeTm[:, hl, p, 0:sz], True,
                    [self.BkeT, self.BqeT], [bba])
            self.tt(self.attT[:sz, h, 0:sz], bka[:sz, 0:sz], tri, ALU.mult, [bba, self.Bconst], [self.BattT])

    def gla_finish(self, obanks, ntok):
        sc, Bsc = self.sc, self.Bsc
        for h in range(4):
            ob, Bo = obanks[h]
            pt, Bp = self.next_pT()
            self.act(pt[:, 0:ntok], ob[:, 0:ntok], AF.Square, [Bo], [Bp])
            bk, bb = self.bankA()
            self.mm(bk[:, 0:ntok], self.onesb[:], pt[:, 0:ntok], True, [Bp, self.Bconst], [bb])
            self.act(self.dsum[:, 0:ntok], bk[:, 0:ntok], AF.Ln, [bb], [self.Bdsum], scale=1.0 / 128, bias=1e-6)
            self.act(self.dsum[:, 0:ntok], self.dsum[:, 0:ntok], AF.Exp, [self.Bdsum], [self.Bdsum], scale=-0.5)
            self.stt(self.osum[:, 0:ntok], ob[:, 0:ntok], self.ggo[:, 0:1], self.dsum[:, 0:ntok], ALU.mult, ALU.mult,
                     [Bo, self.Bdsum, self.Bconst], [self.Bosum])
            self.tt(sc[:, 16 + h, 0:ntok], self.osum[:, 0:ntok], sc[:, 16 + h, 0:ntok], ALU.mult, [self.Bosum, Bsc[16 + h]], [Bsc[16 + h]], eng=POOL)

    def gla_prompt(self, s, i):
        if i == 0:
            for p in range(2):
                self.S.op(DVE, lambda e, p=p: e.memset(self.S32[:, p, :], 0.0), reads=[], writes=[self.BS32[p]])
                self.S.op(DVE, lambda e, p=p: e.memset(self.Sbf[0][:, p, :], 0.0), reads=[], writes=[self.BSbf[0][p]])
        obanks = [self.bankB() for _ in range(4)]
        ofirst = [True] * 4
        for blk in range(4):
            self.gla_block_prep(blk, 128, 0)
            for cl in range(2):
                c = blk * 2 + cl
                cur, nxt = c % 2, (c + 1) % 2
                r0 = 64 * cl
                t0 = blk * 128 + r0
                for h in range(4):
                    p, hl = h // 2, h % 2
                    ob, Bo = obanks[h]
                    self.mm(ob[:, t0:t0 + 64], self.Sbf[cur][:, p, hl * 128:(hl + 1) * 128],
                            self.qeTm[:, hl, p, r0:r0 + 64], ofirst[h], [self.BSbf[cur][p], self.BqeT], [Bo])
                    ofirst[h] = False
                    self.mm(ob[:, t0:t0 + 64], self.gkv[:, blk, 256 + h * 128:256 + (h + 1) * 128], self.attT[:, h, r0:r0 + 64], False,
                            [self.Bgkv[blk], self.BattT], [Bo])
                for p in range(2):
                    bk, bb = self.bankA()
                    self.mm(bk[:, 0:256], self.kdm[:, cl, p * 128:(p + 1) * 128], self.gkv[:, blk, 256 + p * 256:256 + (p + 1) * 256], True,
                            [self.Bkd, self.Bgkv[blk]], [bb])
                    self.stt(self.S32[:, p, :], self.S32[:, p, :], self.eb[:, p, r0 + 63:r0 + 64], bk[:, 0:256], ALU.mult, ALU.add,
                             [self.BS32[p], self.Beb, bb], [self.BS32[p]])
                    self.act(self.Sbf[nxt][:, p, :], self.S32[:, p, :], AF.Copy, [self.BS32[p]], [self.BSbf[nxt][p]])
        self.gla_finish(obanks, NT)
        if i == 3:
            for h in range(4):
                p, hl = h // 2, h % 2
                self.st(self.O["glap"][s, h, :, :], self.S32[64 * hl:64 * hl + 64, p, hl * 128:(hl + 1) * 128], [self.BS32[p]])

    def merge(self, blocks, ntok):
        sc, Bsc = self.sc, self.Bsc
        obase = (0, 16, 12)
        mT = self.uT
        acc = [self.tA[0], self.tA[1], self.tB[0], self.tB[1]]
        Bacc = [self.BtA[0], self.BtA[1], self.BtB[0], self.BtB[1]]
        for half in range(2):
            for b in range(3):
                slot, sbuf = self.unit()
                slotb, sbufb = self.unit()
                for dl in range(4):
                    gb, Bg = self.bankA()
                    for k in range(8):
                        self.mm(gb[:, 0:ntok], slot[:, k, dl * 128:(dl + 1) * 128], self.uT[:, k, 0:ntok], k == 0, [sbuf, self.BuT], [Bg])
                    pt, Bp = self.next_pT()
                    self.act(pt[:, 0:ntok], gb[:, 0:ntok], AF.Sigmoid, [Bg], [Bp])
                    bb_, Bb = self.bankA()
                    for cch in range(4):
                        self.mm(bb_[:, 0:ntok], slotb[:, cch, dl * 128:(dl + 1) * 128], sc[:, obase[b] + cch, 0:ntok], cch == 0,
                                [sbufb, Bsc[obase[b] + cch]], [Bb])
                    a = acc[dl][:, 0:ntok]
                    if b == 0:
                        self.tt(a, bb_[:, 0:ntok], pt[:, 0:ntok], ALU.mult, [Bb, Bp], [Bacc[dl]])
                    else:
                        self.tt(self.osum[:, 0:ntok], bb_[:, 0:ntok], pt[:, 0:ntok], ALU.mult, [Bb, Bp], [self.Bosum])
                        if b == 1:
                            self.tt(a, a, self.osum[:, 0:ntok], ALU.add, [Bacc[dl], self.Bosum], [Bacc[dl]], eng=POOL)
                        else:
                            self.tt(sc[:, 4 + half * 4 + dl, 0:ntok], a, self.osum[:, 0:ntok], ALU.add,
                                    [Bacc[dl], self.Bosum], [Bsc[4 + half * 4 + dl]], eng=POOL)
        for dh in range(2):
            slot, sbuf = self.unit()
            for bi, (tb, sz) in enumerate(blocks):
                bk, bb = self.bankA()
                for dc in range(8):
                    self.mm(bk[:sz, :], sc[:, 4 + dc, tb * 128:tb * 128 + sz], slot[:, dc, 0:512], dc == 0,
                            [sbuf, Bsc[4 + dc]], [bb])
                xs_ = self.xres[:sz, tb, dh * 512:(dh + 1) * 512]
                self.tt(xs_, bk[:sz, :], xs_, ALU.add, [bb, self.Bxres[tb]], [self.Bxres[tb]])

    def final_norm(self, blocks, dram_rows):
        nb = len(blocks)
        for tb, sz in blocks:
            self.act(self.xn[:sz, tb % 2, :], self.xres[:sz, tb, :], AF.Square, [self.Bxres[tb]], [self.Bxn[tb % 2], self.Bss],
                     accum_out=self.ss[:sz, tb:tb + 1])
        sz = blocks[0][1]
        self.act(self.rstd[:sz, 0:nb], self.ss[:sz, 0:nb], AF.Ln, [self.Bss], [self.Brstd], scale=1.0 / D, bias=1e-6)
        self.act(self.rstd[:sz, 0:nb], self.rstd[:sz, 0:nb], AF.Exp, [self.Brstd], [self.Brstd], scale=-0.5)
        for tb, sz in blocks:
            for hf in range(2):
                stg, Bs = self.next_stg()
                cs = slice(hf * 512, (hf + 1) * 512)
                self.stt(stg[:sz, :], self.xres[:sz, tb, cs], self.rstd[:sz, tb:tb + 1], self.gfin[:sz, cs], ALU.mult, ALU.mult,
                         [self.Bxres[tb], self.Brstd, self.Bconst], [Bs])
                self.st(dram_rows[tb * 128:tb * 128 + sz, cs], stg[:sz, :], [Bs])

    def memkv_seq(self, s):
        I, O = self.I, self.O
        blocks = [(0, 128), (1, 128)]
        self.ld(self.xres[:, 0:2, :], I["mem"][s * 256:(s + 1) * 256, :].rearrange("(b p) d -> p b d", p=128), self.Bxres[0:2])
        stage = getattr(self, "stage", 9)
        if stage <= 0:
            return
        self.norm_uT(3, blocks)
        if stage <= 3:
            return
        for half in range(2):
            slot, sbuf = self.unit()
            for mb in range(2):
                bk, bb = self.bankA()
                for k in range(8):
                    self.mm(bk[:, :], self.uT[:, k, mb * 128:(mb + 1) * 128], slot[:, k, 0:512], k == 0, [sbuf, self.BuT], [bb])
                stg, Bs = self.next_stg()
                if stage <= 4:
                    continue
                self.cp(stg[:, 0:512], bk[:, :], [bb], [Bs])
                if stage <= 5:
                    continue
                self.st(O["mkv"][s * 256 + mb * 128:s * 256 + (mb + 1) * 128, half * 512:(half + 1) * 512], stg[:, 0:512], [Bs])
                if stage <= 6:
                    continue
                if half == 1:
                    self.act(self.Vm[:, mb, :], bk[:, :], AF.Copy, [bb], [self.BVm])
                else:
                    pt, Bp = self.next_pT()
                    self.act(pt[:, :], bk[:, :], AF.Copy, [bb], [Bp])
                    bk2, bb2 = self.bankA()
                    bkb = bk2.bitcast(BF16)
                    for h in range(4):
                        self.tr(bkb[:, h * 128:(h + 1) * 128], pt[:, h * 128:(h + 1) * 128], self.identb[:], [Bp], [bb2])
                    self.cp(self.KmT[:, :, mb * 128:(mb + 1) * 128], bkb[:, 0:512].rearrange("p (h n) -> p h n", h=4), [bb2], [self.BKmT])

    def sample_alloc(self):
        big = self.big
        self.kvb = [big[:, i * 1024:(i + 1) * 1024] for i in range(4)]; self.Bkvb = [Buf(f"kvb{i}") for i in range(4)]
        self.ikvb = 0
        self.ktb = [big[:, 4096 + i * 512:4096 + (i + 1) * 512].rearrange("p (h n) -> p h n", h=4) for i in range(4)]; self.Bktb = [Buf(f"ktb{i}") for i in range(4)]
        self.kTn = big[:, 6144:6912].rearrange("p (a n) -> p a n", a=12); self.BkTn = Buf("kTn")
        self.Vn = big[0:64, 6912:8448].rearrange("p (a n) -> p a n", a=3); self.BVn = Buf("Vn")
        self.S0bf = big[:, 8448:12544].rearrange("p (a b n) -> p a b n", a=2, b=16); self.BS0bf = Buf("S0bf")
        self.kds = big[0:64, 12544:12800]; self.Bkds = Buf("kds")
        self.s0f = [big[:, 12800 + i * 256:13056 + i * 256].bitcast(F32) for i in range(2)]; self.Bs0f = [Buf("s0f0"), Buf("s0f1")]
        self.nmask = big[0:64, 13312:13440].rearrange("p (a n) -> p a n", a=2)
        self.s0mask = big[:, 13440:13456]
        self.onehot = big[0:64, 13456:13488].bitcast(F32)
        self.Bsamp = Buf("sampconst")

    def sample_consts(self):
        I = self.I
        B, Bt = self.Bsamp, self.Bxres
        ct = self.xres
        self.ld(ct[0:64, 0, 0:128].rearrange("p (m n) -> p m n", m=2), I["nmask"].rearrange("m p n -> p m n"), Bt, eng=SP)
        self.cp(self.nmask, ct[0:64, 0, 0:128].rearrange("p (m n) -> p m n", m=2), Bt, [B])
        self.ld(ct[:, 1, 0:16], I["s0mask"], Bt, eng=SP)
        self.cp(self.s0mask, ct[:, 1, 0:16], Bt, [B])
        self.ld(self.onehot, I["onehot"], [B], eng=SP)

    def cache_attn(self, s, src_blocks, qchunk, mask, ob, Bo, db, Bd, first):
        sc, Bsc = self.sc, self.Bsc
        nblk = len(src_blocks)
        kv = []
        for (src, c0, nq) in src_blocks:
            i = self.ikvb % 4
            self.ikvb += 1
            kvb, Bkv, ktb, Bkt = self.kvb[i], self.Bkvb[i], self.ktb[i], self.Bktb[i]
            self.ld(kvb[:, :], src, [Bkv], eng=POOL)
            bk, bb = self.bankA()
            bkb = bk.bitcast(BF16)
            for h in range(4):
                self.tr(bkb[:, h * 128:(h + 1) * 128], kvb[:, h * 128:(h + 1) * 128], self.identb[:], [Bkv], [bb])
            self.cp(ktb[:, :, :], bkb[:, 0:512].rearrange("p (h n) -> p h n", h=4), [bb], [Bkt])
            kv.append((kvb, Bkv, ktb, Bkt, c0, nq))
        bk, bb = self.bankA()
        ncol = 0
        cols = []
        f = True
        for (kvb, Bkv, ktb, Bkt, c0, nq) in kv:
            for h in range(4):
                self.mm(bk[:, ncol:ncol + nq], ktb[:, h, :], sc[:, qchunk(h), c0:c0 + nq], f, [Bkt, Bsc[qchunk(h)]], [bb])
                f = False
                cols.append((ncol, nq, h, c0, kvb, Bkv))
                ncol += nq
        if mask is not None:
            self.mm(bk[:, 0:ncol], self.identb[:], mask[:, 0:ncol], False, [self.Bconst, self.Bsamp], [bb])
        pt, Bp = self.next_pT()
        self.act(pt[:, 0:ncol], bk[:, 0:ncol], AF.Exp, [bb], [Bp], scale=SCALE)
        for (cc, nq, h, c0, kvb, Bkv) in cols:
            self.mm(ob[:, h * 64 + c0:h * 64 + c0 + nq], kvb[:, 512 + h * 128:512 + (h + 1) * 128], pt[:, cc:cc + nq], first[0], [Bkv, Bp], [Bo])
            self.mm(db[:, h * 64 + c0:h * 64 + c0 + nq], self.onesb[:], pt[:, cc:cc + nq], first[0], [Bp, self.Bconst], [Bd])
            first[0] = False

    def sample_tile(self):
        I, O = self.I, self.O
        blocks = [(0, 64)]
        n = 64
        sc, Bsc = self.sc, self.Bsc
        sst = getattr(self, "sstage", 99)
        self.ld(self.xres[0:64, 0, :], I["xs"], [self.Bxres[0]])
        self.norm_uT(0, blocks)
        self.ffn(blocks, n)
        if sst <= 1:
            return
        self.norm_uT(1, blocks)
        cosT, snT = self.ropes[:, 0, :], self.ropes[:, 1, :]
        souts = (O["s0s"], O["s1s"], O["s2s"])
        for u in range(13):
            slot, sbuf = self.unit()
            ncol = 512 if u < 12 else 528
            base = u * 512
            c = 0
            while c < ncol:
                col = base + c
                if col < C_V:
                    isk = col >= C_K
                    gh = (col - (C_K if isk else 0)) // 128
                    g, h = gh // 4, gh % 4
                    bk, bb = self.up_chunk(slot, sbuf, c, 128, n)
                    if not isk:
                        self.rope_evac(bk, bb, n, cosT, snT, self.Bconst, lambda src, d=sc[:, g * 4 + h, 0:n]: d if src is None else src, [Bsc[g * 4 + h]])
                    else:
                        tKx, BKx = self.rope_evac(bk, bb, n, cosT, snT, self.Bconst, lambda src, d=self.kTn[:, gh, :]: d if src is None else src, [self.BkTn],
                                                  fp32_out=True)
                        bk2, bb2 = self.bankA()
                        self.tr(bk2[0:64, 0:128], tKx[:, 0:64], self.identf[:], [BKx], [bb2])
                        stg, Bs = self.next_stg()
                        self.cp(stg[0:64, 0:128], bk2[0:64, 0:128], [bb2], [Bs])
                        self.st(souts[g][:, 0, h, :], stg[0:64, 0:128], [Bs])
                    c += 128
                elif col < C_GQ:
                    g = (col - C_V) // 512
                    bk, bb = self.bankA()
                    for k in range(8):
                        self.mm(bk[0:64, :], self.uT[:, k, 0:64], slot[:, k, c:c + 512], k == 0, [sbuf, self.BuT], [bb])
                    self.act(self.Vn[:, g, :], bk[0:64, :], AF.Copy, [bb], [self.BVn])
                    stg, Bs = self.next_stg()
                    self.cp(stg[0:64, 0:512], bk[0:64, :], [bb], [Bs])
                    self.st(souts[g][:, 1, :, :].rearrange("p h e -> p (h e)"), stg[0:64, 0:512], [Bs])
                    c += 512
                elif col < C_GK:
                    p = (col - C_GQ) // 128
                    bk, bb = self.up_chunk(slot, sbuf, c, 128, n)
                    self.cp(self.gqT[:, p, 0:n], bk[:, 0:n], [bb], [self.BgqT])
                    c += 128
                elif col < C_GV:
                    p = (col - C_GK) // 128
                    bk, bb = self.up_chunk(slot, sbuf, c, 128, n)
                    self.cp(self.gkT[:, p, 0:n], bk[:, 0:n], [bb], [self.BgkT])
                    bk, bb = self.bankA()
                    for k in range(8):
                        self.mm(bk[0:64, 0:128], self.uT[:, k, 0:64], slot[:, k, c:c + 128], k == 0, [sbuf, self.BuT], [bb])
                    self.act(self.gkv[0:64, 0, p * 128:(p + 1) * 128], bk[0:64, 0:128], AF.Copy, [bb], [self.Bgkv[0]])
                    c += 128
                elif col < C_GR:
                    bk, bb = self.bankA()
                    for k in range(8):
                        self.mm(bk[0:64, :], self.uT[:, k, 0:64], slot[:, k, c:c + 512], k == 0, [sbuf, self.BuT], [bb])
                    self.act(self.gkv[0:64, 0, 256:768], bk[0:64, :], AF.Copy, [bb], [self.Bgkv[0]])
                    c += 512
                elif col < C_GA:
                    h = (col - C_GR) // 128
                    bk, bb = self.up_chunk(slot, sbuf, c, 128, n)
                    self.act(sc[:, 16 + h, 0:n], bk[:, 0:n], AF.Silu, [bb], [Bsc[16 + h]])
                    c += 128
                elif col < C_MQ:
                    bk, bb = self.up_chunk(slot, sbuf, c, 16, n)
                    self.cp(self.gaT[:, 0:n], bk[0:16, 0:n], [bb], [self.BgaT])
                    c += 16
                else:
                    h = (col - C_MQ) // 128
                    bk, bb = self.up_chunk(slot, sbuf, c, 128, n)
                    self.act(sc[:, 12 + h, 0:n], bk[:, 0:n], AF.Copy, [bb], [Bsc[12 + h]])
                    c += 128
        if sst <= 2:
            return
        ob, Bo = self.bankB()
        db, Bd = self.bankB()
        first = [True]
        for g in range(3):
            for h in range(4):
                bk, bb = self.bankA()
                self.mm(bk[0:64, 0:64], self.kTn[:, g * 4 + h, :], sc[:, g * 4 + h, 0:64], True, [self.BkTn, Bsc[g * 4 + h]], [bb])
                self.mm(bk[0:64, 0:64], self.identb[0:64, 0:64], self.nmask[:, 0 if g == 0 else 1, :], False, [self.Bconst, self.Bsamp], [bb])
                pt, Bp = self.next_pT()
                self.act(pt[0:64, 0:64], bk[0:64, 0:64], AF.Exp, [bb], [Bp], scale=SCALE)
                self.mm(ob[:, h * 64:(h + 1) * 64], self.Vn[:, g, h * 128:(h + 1) * 128], pt[0:64, 0:64], first[0], [self.BVn, Bp], [Bo])
                self.mm(db[:, h * 64:(h + 1) * 64], self.onesb[0:64, :], pt[0:64, 0:64], first[0], [Bp, self.Bconst], [Bd])
                first[0] = False
        if sst <= 3:
            return
        for s in range(16):
            self.cache_attn(s, [(I["c0"][s], 4 * s, 4)], lambda h: h, self.s0mask, ob, Bo, db, Bd, first)
            v1 = I["c1"][s].rearrange("(j r) c -> r j c", r=4)
            self.cache_attn(s, [(v1[t], 4 * s + t, 1) for t in range(4)], lambda h: 4 + h, None, ob, Bo, db, Bd, first)
            v2 = I["c2"][s].rearrange("(j r) c -> r j c", r=16)
            self.cache_attn(s, [(v2[t], 4 * s + t, 1) for t in range(4)], lambda h: 8 + h, None, ob, Bo, db, Bd, first)
        self.S.op(DVE, lambda e, db=db: e.reciprocal(out=self.dsum[:, 0:256], in_=db[:, 0:256]), reads=[Bd], writes=[self.Bdsum])
        for h in range(4):
            self.tt(sc[:, h, 0:64], ob[:, h * 64:(h + 1) * 64], self.dsum[:, h * 64:(h + 1) * 64], ALU.mult, [Bo, self.Bdsum], [Bsc[h]])
        if sst <= 4:
            return
        ob, Bo = self.bankB()
        db, Bd = self.bankB()
        first = [True]
        for s in range(16):
            vm = I["cm"][s].rearrange("(b p) c -> b p c", p=128)
            self.cache_attn(s, [(vm[0], 4 * s, 4), (vm[1], 4 * s, 4)], lambda h: 12 + h, None, ob, Bo, db, Bd, first)
        self.S.op(DVE, lambda e, db=db: e.reciprocal(out=self.dsum[:, 0:256], in_=db[:, 0:256]), reads=[Bd], writes=[self.Bdsum])
        for h in range(4):
            self.tt(sc[:, 12 + h, 0:64], ob[:, h * 64:(h + 1) * 64], self.dsum[:, h * 64:(h + 1) * 64], ALU.mult, [Bo, self.Bdsum], [Bsc[12 + h]])
        if sst <= 5:
            return
        self.gla_block_prep(0, 64, 1)
        if sst == 51:
            return
        for s in range(16):
            for p in range(2):
                j = (2 * s + p) % 2
                self.ld(self.s0f[j][:, :], I["sg"][s, 2 * p:2 * p + 2].rearrange("h k v -> (h k) v"), [self.Bs0f[j]])
                self.act(self.S0bf[:, p, s, :], self.s0f[j][:, :], AF.Copy, [self.Bs0f[j]], [self.BS0bf])
        if sst == 52:
            return
        obanks = [self.bankB() for _ in range(4)]
        for h in range(4):
            p, hl = h // 2, h % 2
            ob, Bo = obanks[h]
            for s in range(16):
                self.mm(ob[:, 4 * s:4 * s + 4], self.S0bf[:, p, s, :], self.qeTm[:, hl, p, 4 * s:4 * s + 4], s == 0,
                        [self.BS0bf, self.BqeT], [Bo])
            self.mm(ob[:, 0:64], self.gkv[0:64, 0, 256 + h * 128:256 + (h + 1) * 128], self.attT[0:64, h, 0:64], False, [self.Bgkv[0], self.BattT], [Bo])
        if sst == 53:
            return
        self.gla_finish(obanks, 64)
        if sst == 54:
            return
        for s in range(16):
            self.S.op(DVE, lambda e, s=s: e.tensor_scalar(out=self.kds[:, :], in0=self.kdm[0:64, 0, :], scalar1=self.onehot[:, s:s + 1], scalar2=None, op0=ALU.mult),
                      reads=[self.Bkd, self.Bconst, self.Bsamp], writes=[self.Bkds])
            for p in range(2):
                bk, bb = self.bankA()
                self.mm(bk[:, 0:256], self.kds[:, p * 128:(p + 1) * 128], self.gkv[0:64, 0, 256 + p * 256:256 + (p + 1) * 256], True, [self.Bkds, self.Bgkv[0]], [bb])
                j = (2 * s + p) % 2
                self.ld(self.s0f[j][:, :], I["sg"][s, 2 * p:2 * p + 2].rearrange("h k v -> (h k) v"), [self.Bs0f[j]])
                stg, Bs = self.next_stg()
                for hl in range(2):
                    r = slice(64 * hl, 64 * hl + 64)
                    self.stt(stg[r, 0:128], self.s0f[j][r, :], self.eb[r, p, 4 * s + 3:4 * s + 4], bk[r, hl * 128:(hl + 1) * 128], ALU.mult, ALU.add,
                             [self.Bs0f[j], self.Beb, bb], [Bs])
                self.st(O["glas"][s, 2 * p:2 * p + 2].rearrange("h k v -> (h k) v"), stg[:, 0:128], [Bs])
        if sst <= 6:
            return
        self.merge(blocks, n)
        if sst <= 7:
            return
        self.norm_uT(2, blocks)
        self.ffn(blocks, n)
        self.final_norm(blocks, O["ys"])

    def build(self):
        self.sample_alloc()
        tile = self.plan_tile()
        self.plan = []
        self.plan += tile
        for s in range(4):
            self.plan += self.plan_memkv()
            for i in range(4):
                self.plan += tile
        self.prologue()
        self.sample_consts()
        self.sample_tile()
        allb = self.Bkvb + self.Bktb + [self.BkTn, self.BVn, self.BS0bf, self.Bkds, self.Bsamp] + self.Bs0f + self.BkT2 + self.BV2
        self.S.op(DVE, lambda e: e.memset(self.ss[:, 0:1], 0.0), reads=[], writes=allb + [self.Bss])
        for s in range(4):
            self.memkv_seq(s)
            for i in range(4):
                self.prompt_tile(s, i)
        self.S.emit()
        return self.nc


_PROG = None


def _get_prog():
    global _PROG
    if _PROG is None:
        _PROG = Prog().build()
    return _PROG


def kernel(x_prompt, x_sample, mem_prompt, cache_swa0, cache_swa1, cache_swa2, cache_mem_kv, state_gla,
           g_ffn1, w_ffn1_in, w_ffn1_out, g_mix, w_in, w_gla_a2, b_gla_a, g_gla_out, w_gate, w_branch,
           w_out, g_mem, w_mem_kv, g_ffn2, w_ffn2_in, w_ffn2_out, g_final):
    f = lambda a: np.ascontiguousarray(np.asarray(a, dtype=np.float32))
    nc = _get_prog()
    masks, _ = _masks()
    k = np.arange(128)[:, None]
    s0mask = np.tile(np.where(k >= np.arange(4)[None, :], 0.0, NEGB), (1, 4)).astype(np.float32)
    c = np.arange(64)
    same = (c[:, None] // 4) == (c[None, :] // 4)
    nm0 = np.where(same & (c[:, None] <= c[None, :]), 0.0, NEGB)
    nm1 = np.where(c[:, None] == c[None, :], 0.0, NEGB)
    nmask = np.stack([nm0, nm1]).astype(np.float32)
    onehot = (c[:, None] // 4 == np.arange(16)[None, :]).astype(np.float32)
    ropep, ropes = _rope_tabs()
    gains = np.stack([f(g)[0].reshape(8, 128).T for g in (g_ffn1, g_mix, g_ffn2, g_mem)], axis=1)
    shared = {
        "w1i": f(w_ffn1_in)[0], "w1o": f(w_ffn1_out)[0], "win": f(w_in)[0], "wg": f(w_gate)[0],
        "wb": f(w_branch)[0].reshape(1536, 1024), "wo": f(w_out)[0], "wm": f(w_mem_kv)[0],
        "w2i": f(w_ffn2_in)[0], "w2o": f(w_ffn2_out)[0], "gains": np.ascontiguousarray(gains),
        "gfin": f(g_final), "wa2": f(w_gla_a2)[0], "ba": f(b_gla_a)[0].reshape(1, 256), "ggo": f(g_gla_out)[0].reshape(128, 1),
        "masks": masks, "smasks": np.zeros((2, 128, 4), np.float32), "gmask": _gmasks(), "ident": np.eye(128, dtype=np.float32),
        "ropep": ropep, "ropes": ropes, "nmask": nmask, "s0mask": s0mask, "onehot": onehot,
    }
    xp, xs, mem = f(x_prompt), f(x_sample), f(mem_prompt)
    c0, c1, c2, cm, sg = f(cache_swa0)[0], f(cache_swa1)[0], f(cache_swa2)[0], f(cache_mem_kv)[0], f(state_gla)[0]
    in_maps = []
    for i in range(NCORE):
        m = dict(shared)
        m["xp"] = xp[4 * i:4 * i + 4].reshape(8192, 1024)
        m["xs"] = xs[16 * i:16 * i + 16].reshape(64, 1024)
        m["mem"] = mem[4 * i:4 * i + 4].reshape(1024, 1024)
        m["c0"] = c0[16 * i:16 * i + 16].reshape(16, 128, 1024)
        m["c1"] = c1[16 * i:16 * i + 16].reshape(16, 512, 1024)
        m["c2"] = c2[16 * i:16 * i + 16].reshape(16, 2048, 1024)
        m["cm"] = cm[16 * i:16 * i + 16].reshape(16, 256, 1024)
        m["sg"] = sg[16 * i:16 * i + 16]
        in_maps.append(m)
    res = run_bass_kernel_spmd(nc, in_maps, core_ids=list(range(NCORE)))
    R = res.results
    cat = lambda n: np.concatenate([np.asarray(r[n]) for r in R], axis=0)
    y_p = cat("yp").reshape(32, 2048, 1024)
    y_s = cat("ys").reshape(128, 4, 1024)
    s0p = cat("s0p")[None]
    s1p = cat("s1p")[None]
    s2p = cat("s2p")[None]
    mkv = cat("mkv").reshape(1, 32, 256, 2, 4, 128)
    glap = cat("glap")[None]
    s0s = cat("s0s").reshape(1, 128, 4, 2, 4, 128)
    s1s = cat("s1s").reshape(1, 128, 4, 2, 4, 128)
    s2s = cat("s2s").reshape(1, 128, 4, 2, 4, 128)
    glas = cat("glas")[None]
    return (y_p, y_s, s0p, s1p, s2p, mkv, glap, s0s, s1s, s2s, glas)
```
